# Optimizing a Trainium2 kernel written in Bass

```python
import jax, jax.numpy as jnp
from jax import lax
import numpy as np


D_MODEL = 1024
BATCH = 4
SEQ = 8192
DEPTH = 4
DEC_BATCH = 16
DEC_SEQ = 4096
PAST_LEN = 128

GRID_W = 64
Q_BLOCK = 128
EPS = 1e-6
ROPE_THETA = 10000.0

MLA_HEADS = 8
MLA_NOPE = 128
MLA_ROPE = 64
MLA_QK = MLA_NOPE + MLA_ROPE
MLA_V = 128
Q_LORA = 384
KV_LORA = 256

GQA_HEADS = 8
GQA_KV_HEADS = 2
GQA_HD = 128

D_FF = 2816

IN_SPLITS = (Q_LORA, KV_LORA, MLA_ROPE, GQA_HEADS * GQA_HD, GQA_KV_HEADS * GQA_HD, GQA_KV_HEADS * GQA_HD, D_MODEL, D_MODEL)
IN_W = Q_LORA + KV_LORA + MLA_ROPE + GQA_HEADS * GQA_HD + 2 * GQA_KV_HEADS * GQA_HD + 2 * D_MODEL

kernel_name = 'hybrid_mla_axial_gqa_macaron_encoder'


def rmsnorm(x, g):
    xf = x.astype(jnp.float32)
    y = xf * lax.rsqrt(jnp.mean(xf * xf, axis=-1, keepdims=True) + EPS)
    return y.astype(x.dtype) * g


def axial_rope_tables(seq, rot_dim, dtype):
    rows = seq // GRID_W
    row = jnp.repeat(jnp.arange(rows, dtype=jnp.float32), GRID_W)
    col = jnp.tile(jnp.arange(GRID_W, dtype=jnp.float32), rows)
    n = rot_dim // 4
    freqs = ROPE_THETA ** (-jnp.arange(n, dtype=jnp.float32) / n)
    ang = jnp.concatenate([row[:, None] * freqs, col[:, None] * freqs], axis=-1)
    cos = jnp.cos(ang)[None, :, None, :].astype(dtype)
    sin = jnp.sin(ang)[None, :, None, :].astype(dtype)
    return cos, sin


def apply_rope(x, cos, sin):
    half = x.shape[-1] // 2
    x1, x2 = x[..., :half], x[..., half:]
    return jnp.concatenate([x1 * cos - x2 * sin, x2 * cos + x1 * sin], axis=-1)


def block_attention(q, k, v, scale):
    B, S, H, dq = q.shape
    Hk = k.shape[2]
    G = H // Hk
    dv = v.shape[-1]
    nb = S // Q_BLOCK
    qb = (q * scale).reshape(B, nb, Q_BLOCK, Hk, G, dq).transpose(1, 0, 2, 3, 4, 5)

    def one_block(q_blk):
        s = jnp.einsum('bqhgd,bkhd->bhgqk', q_blk, k).astype(jnp.float32)
        p = jax.nn.softmax(s, axis=-1).astype(v.dtype)
        return jnp.einsum('bhgqk,bkhe->bqhge', p, v)

    o = lax.map(one_block, qb)
    return o.transpose(1, 0, 2, 3, 4, 5).reshape(B, S, H * dv)


def swiglu(x, w_in, w_out):
    a, b = jnp.split(x @ w_in, 2, axis=-1)
    return (jax.nn.silu(a) * b) @ w_out


def split_columns(proj):
    parts = []
    start = 0
    for w in IN_SPLITS:
        parts.append(proj[..., start:start + w])
        start += w
    return parts


def token_mixing(h, w_in, g_cq, w_uq, g_ckv, w_ukv, g_qn, g_kn, w_o, rope_a, rope_b):
    B, S, _ = h.shape
    c_q, c_kv, k_r, q_b, k_b, v_b, gate_a, gate_b = split_columns(h @ w_in)

    q_a = (rmsnorm(c_q, g_cq) @ w_uq).reshape(B, S, MLA_HEADS, MLA_QK)
    q_a = jnp.concatenate([q_a[..., :MLA_NOPE], apply_rope(q_a[..., MLA_NOPE:], *rope_a)], axis=-1)
    kv_a = (rmsnorm(c_kv, g_ckv) @ w_ukv).reshape(B, S, MLA_HEADS, MLA_NOPE + MLA_V)
    k_r = apply_rope(k_r.reshape(B, S, 1, MLA_ROPE), *rope_a)
    k_a = jnp.concatenate([kv_a[..., :MLA_NOPE], jnp.broadcast_to(k_r, (B, S, MLA_HEADS, MLA_ROPE))], axis=-1)
    v_a = kv_a[..., MLA_NOPE:]
    o_a = block_attention(q_a, k_a, v_a, MLA_QK ** -0.5)

    q_b = apply_rope(rmsnorm(q_b.reshape(B, S, GQA_HEADS, GQA_HD), g_qn), *rope_b)
    k_b = apply_rope(rmsnorm(k_b.reshape(B, S, GQA_KV_HEADS, GQA_HD), g_kn), *rope_b)
    v_b = v_b.reshape(B, S, GQA_KV_HEADS, GQA_HD)
    o_b = block_attention(q_b, k_b, v_b, GQA_HD ** -0.5)

    merged = jax.nn.sigmoid(gate_a) * o_a + jax.nn.sigmoid(gate_b) * o_b
    return merged @ w_o


def trunk(x, norm_ffn1, w_ffn1_in, w_ffn1_out, norm_mix, w_in, g_cq, w_uq, g_ckv, w_ukv,
          g_qn, g_kn, w_o, norm_ffn2, w_ffn2_in, w_ffn2_out, norm_final):
    S = x.shape[1]
    rope_a = axial_rope_tables(S, MLA_ROPE, x.dtype)
    rope_b = axial_rope_tables(S, GQA_HD, x.dtype)
    for l in range(DEPTH):
        x = x + 0.5 * swiglu(rmsnorm(x, norm_ffn1[l]), w_ffn1_in[l], w_ffn1_out[l])
        h = rmsnorm(x, norm_mix[l])
        x = x + token_mixing(h, w_in[l], g_cq[l], w_uq[l], g_ckv[l], w_ukv[l],
                             g_qn[l], g_kn[l], w_o[l], rope_a, rope_b)
        x = x + 0.5 * swiglu(rmsnorm(x, norm_ffn2[l]), w_ffn2_in[l], w_ffn2_out[l])
    return rmsnorm(x, norm_final)


def setup_inputs(seed: int = 0) -> dict:
    key = jax.random.key(seed)
    ks = jax.random.split(key, 20)
    f32 = jnp.float32

    def w(k, shape):
        return jax.random.normal(k, shape, f32) * (shape[-2] ** -0.5)

    def gain(k, shape):
        return 1.0 + 0.05 * jax.random.normal(k, shape, f32)

    return {
        'x_prompt': jax.random.normal(ks[0], (BATCH, SEQ, D_MODEL), f32),
        'x_sample': jax.random.normal(ks[1], (DEC_BATCH, DEC_SEQ, D_MODEL), f32),
        'norm_ffn1': gain(ks[2], (DEPTH, D_MODEL)),
        'w_ffn1_in': w(ks[3], (DEPTH, D_MODEL, 2 * D_FF)),
        'w_ffn1_out': w(ks[4], (DEPTH, D_FF, D_MODEL)),
        'norm_mix': gain(ks[5], (DEPTH, D_MODEL)),
        'w_in': w(ks[6], (DEPTH, D_MODEL, IN_W)),
        'g_cq': gain(ks[7], (DEPTH, Q_LORA)),
        'w_uq': w(ks[8], (DEPTH, Q_LORA, MLA_HEADS * MLA_QK)),
        'g_ckv': gain(ks[9], (DEPTH, KV_LORA)),
        'w_ukv': w(ks[10], (DEPTH, KV_LORA, MLA_HEADS * (MLA_NOPE + MLA_V))),
        'g_qn': gain(ks[11], (DEPTH, GQA_HD)),
        'g_kn': gain(ks[12], (DEPTH, GQA_HD)),
        'w_o': w(ks[13], (DEPTH, D_MODEL, D_MODEL)),
        'norm_ffn2': gain(ks[14], (DEPTH, D_MODEL)),
        'w_ffn2_in': w(ks[15], (DEPTH, D_MODEL, 2 * D_FF)),
        'w_ffn2_out': w(ks[16], (DEPTH, D_FF, D_MODEL)),
        'norm_final': gain(ks[17], (D_MODEL,)),
    }


def reference(x_prompt, x_sample, norm_ffn1, w_ffn1_in, w_ffn1_out, norm_mix, w_in, g_cq, w_uq,
              g_ckv, w_ukv, g_qn, g_kn, w_o, norm_ffn2, w_ffn2_in, w_ffn2_out, norm_final):
    y_prompt = trunk(x_prompt, norm_ffn1, w_ffn1_in, w_ffn1_out, norm_mix, w_in, g_cq, w_uq,
                     g_ckv, w_ukv, g_qn, g_kn, w_o, norm_ffn2, w_ffn2_in, w_ffn2_out, norm_final)
    y_sample = trunk(x_sample, norm_ffn1, w_ffn1_in, w_ffn1_out, norm_mix, w_in, g_cq, w_uq,
                     g_ckv, w_ukv, g_qn, g_kn, w_o, norm_ffn2, w_ffn2_in, w_ffn2_out, norm_final)
    return (y_prompt, y_sample)
```

```python
import numpy as np
from contextlib import ExitStack
import concourse.bass as bass
import concourse.mybir as mybir
from concourse.bass_utils import run_bass_kernel_spmd

F32 = mybir.dt.float32
BF16 = mybir.dt.bfloat16
ALU = mybir.AluOpType
AF = mybir.ActivationFunctionType

D = 1024
DFF = 2816
NCH = 8
NJ = DFF // 128
INW = 4288
EPS = 1e-6
TB = 512
N_CORES = 8
C_CQ, C_CKV, C_KR, C_QB, C_KB, C_VB, C_GA, C_GB = 0, 384, 640, 704, 1728, 1984, 2240, 3264


class Buf:
    __slots__ = ("t", "w", "r", "ds", "name")

    def __init__(self, t, name=""):
        self.t = t
        self.w = {}
        self.r = {}
        self.ds = None
        self.name = name

    def __getitem__(self, k):
        return self.t[k]


class Ctx:
    NDS = 84

    def __init__(self, nc, es):
        self.nc = nc
        self.E = {"pe": nc.tensor, "act": nc.scalar, "dve": nc.vector, "pool": nc.gpsimd, "sp": nc.sync}
        self.esem = {k: es.enter_context(nc.semaphore("s_" + k)) for k in ("pe", "act", "dve", "pool")}
        self.ecnt = {k: 0 for k in self.esem}
        self.dsem = [es.enter_context(nc.semaphore("d%d" % i)) for i in range(self.NDS)]
        self.dcnt = [0] * self.NDS
        self.dnext = 0
        self.waited = {}
        self.n_ins = 0

    def _sem(self, key):
        return self.esem[key] if isinstance(key, str) else self.dsem[key[1]]

    def wait(self, eng, toks):
        need = {}
        for k, v in toks:
            if v > need.get(k, 0):
                need[k] = v
        for k, v in need.items():
            if self.waited.get((eng, k), 0) >= v:
                continue
            self.E[eng].wait_ge(self._sem(k), v)
            self.waited[(eng, k)] = v

    def alloc_ds(self, buf):
        if buf.ds is None:
            assert self.dnext < self.NDS, "out of DMA semaphores"
            buf.ds = self.dnext
            self.dnext += 1
        return buf.ds

    def _deps(self, eng, reads, writes):
        toks = []
        for b in reads:
            for k, v in b.w.items():
                if eng == "pe" and k == "pe":
                    continue
                toks.append((k, v))
        for b in writes:
            for k, v in b.w.items():
                if k == eng:
                    continue
                toks.append((k, v))
            for k, v in b.r.items():
                if k == eng:
                    continue
                toks.append((k, v))
        return toks

    def _commit(self, key, val, reads, writes):
        for b in reads:
            if b.r.get(key, 0) < val:
                b.r[key] = val
        for b in writes:
            if b.w.get(key, 0) < val:
                b.w[key] = val

    def op(self, eng, fn, reads=(), writes=()):
        self.wait(eng, self._deps(eng, reads, writes))
        ins = fn(self.E[eng])
        self.ecnt[eng] += 1
        ins.then_inc(self.esem[eng], 1)
        self._commit(eng, self.ecnt[eng], reads, writes)
        self.n_ins += 1
        return ins

    def mm(self, out_buf, out_ap, pairs, reads, start=True, stop=True):
        self.wait("pe", self._deps("pe", reads, [out_buf]))
        n = len(pairs)
        ins = None
        for i, (l, r) in enumerate(pairs):
            ins = self.nc.tensor.matmul(out_ap, l, r, start=(start and i == 0), stop=(stop and i == n - 1))
        self.ecnt["pe"] += 1
        ins.then_inc(self.esem["pe"], 1)
        self._commit("pe", self.ecnt["pe"], reads, [out_buf])
        self.n_ins += n

    def tr(self, out_buf, out_ap, in_ap, ident_ap, reads):
        self.wait("pe", self._deps("pe", reads, [out_buf]))
        ins = self.nc.tensor.transpose(out_ap, in_ap, ident_ap)
        self.ecnt["pe"] += 1
        ins.then_inc(self.esem["pe"], 1)
        self._commit("pe", self.ecnt["pe"], reads, [out_buf])
        self.n_ins += 1

    def dma(self, q, out_ap, in_ap, sb, load):
        i = self.alloc_ds(sb)
        key = ("d", i)
        if load:
            toks = [(k, v) for k, v in list(sb.w.items()) + list(sb.r.items()) if k != key]
        else:
            toks = [(k, v) for k, v in sb.w.items() if k != key]
        self.wait(q, toks)
        self.dcnt[i] += 16
        self.E[q].dma_start(out=out_ap, in_=in_ap).then_inc(self.dsem[i], 16)
        if load:
            sb.w[key] = self.dcnt[i]
        else:
            sb.r[key] = self.dcnt[i]
        self.n_ins += 1

    def barrier(self):
        toks = [(k, v) for k, v in self.ecnt.items() if v > 0]
        toks += [(("d", i), self.dcnt[i]) for i in range(self.NDS) if self.dcnt[i] > 0]
        for eng in ("sp", "pe", "act", "dve", "pool"):
            self.wait(eng, toks)
        self.dnext = 0


def build_program(SEG, DEPTH, phases=None):
    T = 3 * SEG
    NB = T // TB
    NT = T // 128
    nc = bass.Bass("TRN2", target_bir_lowering=False)
    LW = max(DEPTH, 1)

    def din(name, shape, dt=F32):
        return nc.dram_tensor(name, list(shape), dt, kind="ExternalInput").ap()

    def dscr(name, shape, dt):
        return nc.dram_tensor(name, list(shape), dt, kind="Internal").ap()

    xin = din("xin", [T, D])
    Wd = {
        "norm_ffn1": din("norm_ffn1", [LW, 128, NCH]),
        "w_ffn1_in": din("w_ffn1_in", [LW, D, 2 * DFF]),
        "w_ffn1_out": din("w_ffn1_out", [LW, DFF, D]),
        "norm_mix": din("norm_mix", [LW, 128, NCH]),
        "w_in": din("w_in", [LW, D, INW]),
        "g_cq": din("g_cq", [LW, 128, 3]),
        "w_uq": din("w_uq", [LW, 384, 1536]),
        "g_ckv": din("g_ckv", [LW, 128, 2]),
        "w_ukv": din("w_ukv", [LW, 256, 2048]),
        "g_qn": din("g_qn", [LW, 128, 1]),
        "g_kn": din("g_kn", [LW, 128, 1]),
        "w_o": din("w_o", [LW, D, D]),
        "norm_ffn2": din("norm_ffn2", [LW, 128, NCH]),
        "w_ffn2_in": din("w_ffn2_in", [LW, D, 2 * DFF]),
        "w_ffn2_out": din("w_ffn2_out", [LW, DFF, D]),
        "norm_final": din("norm_final", [128, D]),
    }
    ropeA = din("ropeA", [2, 64, T])
    ropeB = din("ropeB", [2, 128, T])
    maskb_d = din("maskb", [128, 2])
    ident_d = din("ident", [128, 128])
    yout = nc.dram_tensor("yout", [T, D], F32, kind="ExternalOutput").ap()

    xT_d = dscr("xT_d", [NCH, 128, T], F32)
    QaN_d = dscr("QaN_d", [8, 128, T], BF16)
    QaR_d = dscr("QaR_d", [8, 64, T], BF16)
    KaN_d = dscr("KaN_d", [8, 128, T], BF16)
    KR_d = dscr("KR_d", [64, T], BF16)
    Va_d = dscr("Va_d", [8, 128, NT, 128], BF16)
    Qb_d = dscr("Qb_d", [8, 128, T], BF16)
    Kb_d = dscr("Kb_d", [2, 128, T], BF16)
    Vb_d = dscr("Vb_d", [2, 128, NT, 128], BF16)
    Ga_d = dscr("Ga_d", [8, 128, T], BF16)
    Gb_d = dscr("Gb_d", [8, 128, T], BF16)
    Oa_d = dscr("Oa_d", [8, 128, T], BF16)
    Ob_d = dscr("Ob_d", [8, 128, T], BF16)

    with ExitStack() as es:
        cx = Ctx(nc, es)

        uniq = [0]

        def sb(es_, name, shape, dt):
            uniq[0] += 1
            return Buf(es_.enter_context(nc.sbuf_tensor("sb%d_%s" % (uniq[0], name), list(shape), dt)), name)

        ps_t = es.enter_context(nc.psum_tensor("ps", [128, 8 * 512], F32))
        banks = [Buf(ps_t, "bank%d" % i) for i in range(8)]

        def bk(i, rows=128, c0=0, c1=512):
            return ps_t[0:rows, i * 512 + c0:i * 512 + c1]

        ident = sb(es, "ident", [128, 128], F32)
        ones = sb(es, "ones", [128, 128], BF16)
        epst = sb(es, "epst", [128, 1], F32)
        maskb = sb(es, "maskb", [128, 2], F32)
        cx.dma("sp", ident[:], ident_d[:, :], ident, True)
        cx.dma("sp", maskb[:], maskb_d[:, :], maskb, True)
        cx.op("dve", lambda e: e.memset(ones[:], 1.0), writes=[ones])
        cx.op("dve", lambda e: e.memset(epst[:], EPS), writes=[epst])

        def xT_blk(b):
            return xT_d[:, :, b * TB:(b + 1) * TB].rearrange("c p t -> p c t")

        def phase_X0():
            with ExitStack() as pe_:
                xt = [sb(pe_, "x0_in%d" % i, [128, D], F32) for i in range(2)]
                xo = [sb(pe_, "x0_out%d" % i, [128, NCH, TB], F32) for i in range(2)]
                k = 0
                for b in range(NB):
                    for t in range(4):
                        xi = xt[k % 2]
                        k += 1
                        r0 = b * TB + t * 128
                        cx.dma("sp", xi[:], xin[r0:r0 + 128, :], xi, True)
                        for c in range(NCH):
                            cx.tr(banks[c], bk(c, 128, t * 128, (t + 1) * 128), xi[:, c * 128:(c + 1) * 128],
                                  ident[:], [xi, ident])
                    o = xo[b % 2]
                    for c in range(NCH):
                        eng = "act" if c % 2 else "dve"
                        if eng == "act":
                            cx.op("act", lambda e, c=c: e.copy(out=o[:, c, :], in_=bk(c)), reads=[banks[c]], writes=[o])
                        else:
                            cx.op("dve", lambda e, c=c: e.tensor_copy(out=o[:, c, :], in_=bk(c)), reads=[banks[c]], writes=[o])
                    cx.dma("sp", xT_blk(b), o[:], o, False)
            cx.barrier()

        def phase_XF():
            with ExitStack() as pe_:
                xb = [sb(pe_, "xf_in%d" % i, [128, NCH, TB], F32) for i in range(2)]
                yt = [sb(pe_, "xf_y%d" % i, [128, D], F32) for i in range(2)]
                junk = sb(pe_, "xf_junk", [128, D], F32)
                gfin = sb(pe_, "xf_g", [128, D], F32)
                ssum = [sb(pe_, "xf_ss%d" % i, [128, 1], F32) for i in range(2)]
                rs = [sb(pe_, "xf_rs%d" % i, [128, 1], F32) for i in range(2)]
                cx.dma("sp", gfin[:], Wd["norm_final"][:, :], gfin, True)
                cx.dma("sp", xb[0][:], xT_blk(0), xb[0], True)
                k = 0
                for b in range(NB):
                    if b + 1 < NB:
                        cx.dma("sp", xb[(b + 1) % 2][:], xT_blk(b + 1), xb[(b + 1) % 2], True)
                    x = xb[b % 2]
                    for t in range(4):
                        par = k % 2
                        k += 1
                        b0, b1 = banks[2 * par], banks[2 * par + 1]
                        for c in range(NCH):
                            bb = b0 if c < 4 else b1
                            col = (2 * par) * 512 + c * 128
                            cx.tr(bb, ps_t[:, col:col + 128], x[:, c, t * 128:(t + 1) * 128], ident[:], [x, ident])
                        xps = ps_t[:, (2 * par) * 512:(2 * par) * 512 + D]
                        s_, r_, y = ssum[par], rs[par], yt[par]
                        cx.op("pool", lambda e: e.memset(s_[:], 0.0), writes=[s_])
                        cx.op("act", lambda e: e.activation(out=junk[:], in_=xps, func=AF.Square, accum_out=s_[:]),
                              reads=[b0, b1], writes=[junk, s_])
                        cx.op("act", lambda e: e.activation(out=r_[:], in_=s_[:], func=AF.Sqrt, bias=epst[:, 0:1],
                                                            scale=1.0 / D), reads=[s_, epst], writes=[r_])
                        cx.op("dve", lambda e: e.reciprocal(out=r_[:], in_=r_[:]), reads=[r_], writes=[r_])
                        cx.op("dve", lambda e: e.scalar_tensor_tensor(out=y[:], in0=xps, scalar=r_[:, 0:1], in1=gfin[:],
                                                                      op0=ALU.mult, op1=ALU.mult),
                              reads=[b0, b1, r_, gfin], writes=[y])
                        r0 = b * TB + t * 128
                        cx.dma("sp", yout[r0:r0 + 128, :], y[:], y, False)
            cx.barrier()

        def emit_norm(x, gain, hT, sq, rsb, ssbank, ssi):
            for c in range(NCH):
                s = sq[c % 2]
                cx.op("act", lambda e, c=c, s=s: e.activation(out=s[:], in_=x[:, c, :], func=AF.Square),
                      reads=[x], writes=[s])
                cx.mm(ssbank, bk(ssi), [(ones[:], s[:])], [ones, s], start=(c == 0), stop=(c == NCH - 1))
            cx.op("act", lambda e: e.activation(out=rsb[:], in_=bk(ssi), func=AF.Sqrt, bias=epst[:, 0:1], scale=1.0 / D),
                  reads=[ssbank, epst], writes=[rsb])
            cx.op("dve", lambda e: e.reciprocal(out=rsb[:], in_=rsb[:]), reads=[rsb], writes=[rsb])
            for c in range(NCH):
                cx.op("dve", lambda e, c=c: e.scalar_tensor_tensor(out=hT[:, c, :], in0=x[:, c, :], scalar=gain[:, c:c + 1],
                                                                   in1=rsb[:], op0=ALU.mult, op1=ALU.mult),
                      reads=[x, gain, rsb], writes=[hT])

        def phase_F(l, which):
            w_in_d = Wd["w_ffn%d_in" % which][l]
            w_out_d = Wd["w_ffn%d_out" % which][l]
            gain_d = Wd["norm_ffn%d" % which][l]
            with ExitStack() as pe_:
                W1 = sb(pe_, "f_w1", [128, NCH, 2 * DFF], BF16)
                W2 = sb(pe_, "f_w2", [128, NJ, D], BF16)
                gain = sb(pe_, "f_g", [128, NCH], F32)
                xb = [sb(pe_, "f_x%d" % i, [128, NCH, TB], F32) for i in range(2)]
                hT = sb(pe_, "f_h", [128, NCH, TB], BF16)
                gT = sb(pe_, "f_gt", [128, NJ, TB], BF16)
                sq = [sb(pe_, "f_sq%d" % i, [128, TB], BF16) for i in range(2)]
                rsb = sb(pe_, "f_rs", [128, TB], F32)
                sl = [sb(pe_, "f_sl%d" % i, [128, TB], F32) for i in range(2)]
                cx.dma("sp", gain[:], gain_d[:, :], gain, True)
                cx.dma("sp", xb[0][:], xT_blk(0), xb[0], True)
                for kc in range(NCH):
                    for hf in range(2):
                        cx.dma("pool", W1[:, kc, hf * DFF:(hf + 1) * DFF],
                               w_in_d[kc * 128:(kc + 1) * 128, hf * DFF:(hf + 1) * DFF], W1, True)
                for j in range(NJ):
                    cx.dma("pool", W2[:, j, :], w_out_d[j * 128:(j + 1) * 128, :], W2, True)
                A_ = (0, 1)
                B_ = (2, 3)
                SS = 4
                O_ = (5, 6)
                emit_norm(xb[0], gain, hT, sq, rsb, banks[SS], SS)
                for b in range(NB):
                    x = xb[b % 2]
                    if b + 1 < NB:
                        cx.dma("sp", xb[(b + 1) % 2][:], xT_blk(b + 1), xb[(b + 1) % 2], True)
                    for j in range(NJ):
                        ia, ib = A_[j % 2], B_[j % 2]
                        cx.mm(banks[ia], bk(ia), [(W1[:, kc, j * 128:(j + 1) * 128], hT[:, kc, :]) for kc in range(NCH)],
                              [W1, hT])
                        cx.mm(banks[ib], bk(ib),
                              [(W1[:, kc, DFF + j * 128:DFF + (j + 1) * 128], hT[:, kc, :]) for kc in range(NCH)], [W1, hT])
                        s = sl[j % 2]
                        cx.op("act", lambda e, s=s, ia=ia: e.activation(out=s[:], in_=bk(ia), func=AF.Silu),
                              reads=[banks[ia]], writes=[s])
                        cx.op("dve", lambda e, s=s, ib=ib, j=j: e.tensor_tensor(out=gT[:, j, :], in0=bk(ib), in1=s[:],
                                                                                op=ALU.mult),
                              reads=[banks[ib], s], writes=[gT])
                    for c in range(NCH):
                        io = O_[c % 2]
                        cx.mm(banks[io], bk(io), [(W2[:, j, c * 128:(c + 1) * 128], gT[:, j, :]) for j in range(NJ)],
                              [W2, gT])
                        cx.op("dve", lambda e, c=c, io=io: e.scalar_tensor_tensor(out=x[:, c, :], in0=bk(io), scalar=0.5,
                                                                                  in1=x[:, c, :], op0=ALU.mult, op1=ALU.add),
                              reads=[banks[io], x], writes=[x])
                        if c == 3 and b + 1 < NB:
                            emit_norm(xb[(b + 1) % 2], gain, hT, sq, rsb, banks[SS], SS)
                    cx.dma("sp", xT_blk(b), x[:], x, False)
            cx.barrier()


        def evac(eng, out_ap, in_ap, reads, writes):
            if eng == "act":
                cx.op("act", lambda e: e.copy(out=out_ap, in_=in_ap), reads=reads, writes=writes)
            else:
                cx.op(eng, lambda e: e.tensor_copy(out=out_ap, in_=in_ap), reads=reads, writes=writes)

        def phase_P(l):
            w_in_d = Wd["w_in"][l]
            with ExitStack() as pe_:
                Win = sb(pe_, "p_win", [128, NCH, INW], BF16)
                Wuq = sb(pe_, "p_wuq", [128, 3, 1536], BF16)
                Wukv = sb(pe_, "p_wukv", [128, 2, 2048], BF16)
                gain = sb(pe_, "p_g", [128, NCH], F32)
                gcq = sb(pe_, "p_gcq", [128, 3], F32)
                gckv = sb(pe_, "p_gckv", [128, 2], F32)
                gqn = sb(pe_, "p_gqn", [128, 1], F32)
                gkn = sb(pe_, "p_gkn", [128, 1], F32)
                xb1 = sb(pe_, "p_x", [128, NCH, TB], F32)
                xb = [xb1, xb1]
                hTs = [sb(pe_, "p_h%d" % i, [128, NCH, TB], BF16) for i in range(2)]
                sq = [sb(pe_, "p_sq%d" % i, [128, TB], BF16) for i in range(2)]
                sqq = sb(pe_, "p_sqq", [128, 5, TB], BF16)
                rsb = sb(pe_, "p_rs", [128, TB], F32)
                rsq = [sb(pe_, "p_rsq%d" % i, [128, TB], F32) for i in range(2)]
                cqf = sb(pe_, "p_cqf", [128, 5, TB], F32)
                cqn = sb(pe_, "p_cqn", [128, 3, TB], BF16)
                ckvn = sb(pe_, "p_ckvn", [128, 2, TB], BF16)
                rA = [sb(pe_, "p_ra%d" % i, [64, 2, TB], F32) for i in range(2)]
                rB = [sb(pe_, "p_rb%d" % i, [128, 2, TB], F32) for i in range(2)]
                uu = [sb(pe_, "p_u%d" % i, [128, TB], F32) for i in range(2)]
                us = [sb(pe_, "p_us%d" % i, [128, TB], F32) for i in range(2)]
                t1 = [sb(pe_, "p_t1%d" % i, [128, TB], F32) for i in range(2)]
                t2 = [sb(pe_, "p_t2%d" % i, [128, TB], F32) for i in range(2)]
                NR = 6
                ring = [sb(pe_, "p_ring%d" % i, [128, TB], BF16) for i in range(NR)]
                Vst = sb(pe_, "p_vst", [128, 8, 4, 128], BF16)
                Vbst = sb(pe_, "p_vbst", [128, 2, 4, 128], BF16)
                st = {"ring": 0, "mb": 0, "ss": 0, "rp": 0, "sq": 0}

                for (g_sb, g_d) in ((gain, "norm_mix"), (gcq, "g_cq"), (gckv, "g_ckv"), (gqn, "g_qn"), (gkn, "g_kn")):
                    cx.dma("sp", g_sb[:], Wd[g_d][l][:, :], g_sb, True)
                cx.dma("sp", xb[0][:], xT_blk(0), xb[0], True)
                HW = INW // 2
                for kc in range(NCH):
                    for hf in range(2):
                        cx.dma("pool", Win[:, kc, hf * HW:(hf + 1) * HW],
                               w_in_d[kc * 128:(kc + 1) * 128, hf * HW:(hf + 1) * HW], Win, True)
                for kc in range(3):
                    cx.dma("pool", Wuq[:, kc, :], Wd["w_uq"][l][kc * 128:(kc + 1) * 128, :], Wuq, True)
                for kc in range(2):
                    for hf in range(2):
                        cx.dma("pool", Wukv[:, kc, hf * 1024:(hf + 1) * 1024],
                               Wd["w_ukv"][l][kc * 128:(kc + 1) * 128, hf * 1024:(hf + 1) * 1024], Wukv, True)

                def next_mb():
                    i = st["mb"] % 4
                    st["mb"] += 1
                    return i

                def next_ss():
                    i = 4 + st["ss"] % 2
                    st["ss"] += 1
                    return i

                def next_ring():
                    r = ring[st["ring"] % NR]
                    st["ring"] += 1
                    return r

                def load_rope(b):
                    cx.dma("sp", rA[b % 2][:], ropeA[:, :, b * TB:(b + 1) * TB].rearrange("k p t -> p k t"), rA[b % 2], True)
                    cx.dma("sp", rB[b % 2][:], ropeB[:, :, b * TB:(b + 1) * TB].rearrange("k p t -> p k t"), rB[b % 2], True)

                def rope(src_ap_fn, src_buf, src_psum, R, tab, out_ap, out_buf):
                    k = st["rp"] % 2
                    st["rp"] += 1
                    H2 = R // 2
                    u_s, a_, b_ = us[k], t1[k], t2[k]
                    ce = "dve" if src_psum else "pool"
                    evac(ce, u_s[0:H2, :], src_ap_fn(H2, R), [src_buf], [u_s])
                    evac(ce, u_s[H2:R, :], src_ap_fn(0, H2), [src_buf], [u_s])
                    cx.op("dve", lambda e: e.tensor_tensor(out=a_[0:R, :], in0=src_ap_fn(0, R), in1=tab[0:R, 0, :], op=ALU.mult),
                          reads=[src_buf, tab], writes=[a_])
                    cx.op("pool", lambda e: e.tensor_tensor(out=b_[0:R, :], in0=u_s[0:R, :], in1=tab[0:R, 1, :], op=ALU.mult),
                          reads=[u_s, tab], writes=[b_])
                    cx.op("dve", lambda e: e.tensor_tensor(out=out_ap, in0=a_[0:R, :], in1=b_[0:R, :], op=ALU.add),
                          reads=[a_, b_], writes=[out_buf])

                def win_mm(i, rows, col0, hT, ncols=None):
                    cx.mm(banks[i], bk(i, rows), [(Win[:, kc, col0:col0 + rows], hT[:, kc, :]) for kc in range(NCH)], [Win, hT])

                load_rope(0)
                emit_norm(xb[0], gain, hTs[0], sq, rsb, banks[4], 4)
                if NB > 1:
                    cx.dma("sp", xb1[:], xT_blk(1), xb1, True)
                st["ss"] = 1
                for b in range(NB):
                    t0 = b * TB
                    x, hT = xb[b % 2], hTs[b % 2]
                    tA, tB_ = rA[b % 2], rB[b % 2]
                    if b + 1 < NB:
                        load_rope(b + 1)
                    for ci in range(5):
                        i = next_mb()
                        win_mm(i, 128, ci * 128, hT)
                        evac("act", cqf[:, ci, :], bk(i), [banks[i]], [cqf])
                        cx.op("act", lambda e, i=i, ci=ci: e.activation(out=sqq[:, ci, :], in_=bk(i), func=AF.Square),
                              reads=[banks[i]], writes=[sqq])
                    i = next_mb()
                    win_mm(i, 64, C_KR, hT)
                    r = next_ring()
                    rope(lambda a, b_, i=i: ps_t[a:b_, i * 512:(i + 1) * 512], banks[i], True, 64, tA, r[0:64, :], r)
                    cx.dma("sp", KR_d[:, t0:t0 + TB], r[0:64, :], r, False)

                    def gate(gi):
                        i = next_mb()
                        win_mm(i, 128, C_GA + gi * 128, hT)
                        r = next_ring()
                        evac("act" if gi % 2 else "dve", r[:], bk(i), [banks[i]], [r])
                        dst = Ga_d if gi < 8 else Gb_d
                        cx.dma("sp", dst[gi % 8, :, t0:t0 + TB], r[:], r, False)

                    for gi in range(4):
                        gate(gi)
                    iq, ikv = next_ss(), next_ss()
                    cx.mm(banks[iq], bk(iq), [(ones[:], sqq[:, c, :]) for c in range(3)], [ones, sqq])
                    cx.mm(banks[ikv], bk(ikv), [(ones[:], sqq[:, 3 + c, :]) for c in range(2)], [ones, sqq])
                    cx.op("act", lambda e: e.activation(out=rsq[0][:], in_=bk(iq), func=AF.Sqrt, bias=epst[:, 0:1], scale=1.0 / 384),
                          reads=[banks[iq], epst], writes=[rsq[0]])
                    cx.op("act", lambda e: e.activation(out=rsq[1][:], in_=bk(ikv), func=AF.Sqrt, bias=epst[:, 0:1], scale=1.0 / 256),
                          reads=[banks[ikv], epst], writes=[rsq[1]])
                    cx.op("dve", lambda e: e.reciprocal(out=rsq[0][:], in_=rsq[0][:]), reads=[rsq[0]], writes=[rsq[0]])
                    cx.op("dve", lambda e: e.reciprocal(out=rsq[1][:], in_=rsq[1][:]), reads=[rsq[1]], writes=[rsq[1]])
                    for c in range(3):
                        cx.op("dve", lambda e, c=c: e.scalar_tensor_tensor(out=cqn[:, c, :], in0=cqf[:, c, :], scalar=gcq[:, c:c + 1],
                                                                           in1=rsq[0][:], op0=ALU.mult, op1=ALU.mult),
                              reads=[cqf, gcq, rsq[0]], writes=[cqn])
                    for c in range(2):
                        cx.op("dve", lambda e, c=c: e.scalar_tensor_tensor(out=ckvn[:, c, :], in0=cqf[:, 3 + c, :], scalar=gckv[:, c:c + 1],
                                                                           in1=rsq[1][:], op0=ALU.mult, op1=ALU.mult),
                              reads=[cqf, gckv, rsq[1]], writes=[ckvn])
                    for gi in range(4, 16):
                        gate(gi)
                    for h in range(8):
                        i = next_mb()
                        cx.mm(banks[i], bk(i), [(Wuq[:, kc, h * 192:h * 192 + 128], cqn[:, kc, :]) for kc in range(3)], [Wuq, cqn])
                        r = next_ring()
                        evac("act", r[:], bk(i), [banks[i]], [r])
                        cx.dma("sp", QaN_d[h, :, t0:t0 + TB], r[:], r, False)
                        i = next_mb()
                        cx.mm(banks[i], bk(i, 64), [(Wuq[:, kc, h * 192 + 128:h * 192 + 192], cqn[:, kc, :]) for kc in range(3)],
                              [Wuq, cqn])
                        r = next_ring()
                        rope(lambda a, b_, i=i: ps_t[a:b_, i * 512:(i + 1) * 512], banks[i], True, 64, tA, r[0:64, :], r)
                        cx.dma("sp", QaR_d[h, :, t0:t0 + TB], r[0:64, :], r, False)
                    if b + 1 < NB:
                        st["ss"] = 0
                        emit_norm(xb[(b + 1) % 2], gain, hTs[(b + 1) % 2], sq, rsb, banks[4], 4)
                        st["ss"] = 1
                        if b + 2 < NB:
                            cx.dma("sp", xb1[:], xT_blk(b + 2), xb1, True)
                    for h in range(8):
                        i = next_mb()
                        cx.mm(banks[i], bk(i), [(Wukv[:, kc, h * 256:h * 256 + 128], ckvn[:, kc, :]) for kc in range(2)], [Wukv, ckvn])
                        r = next_ring()
                        evac("act" if h % 2 else "dve", r[:], bk(i), [banks[i]], [r])
                        cx.dma("sp", KaN_d[h, :, t0:t0 + TB], r[:], r, False)
                    for t in range(4):
                        for hg in range(2):
                            i = next_mb()
                            cx.mm(banks[i], bk(i).rearrange("p (h d) -> p h d", d=128),
                                  [(ckvn[:, kc, t * 128:(t + 1) * 128],
                                    Wukv.t[:, kc, :].rearrange("p (h x) -> p h x", x=256)[:, hg * 4:(hg + 1) * 4, 128:256])
                                   for kc in range(2)], [Wukv, ckvn])
                            evac("act" if hg else "dve", Vst[:, hg * 4:(hg + 1) * 4, t, :], bk(i).rearrange("p (h d) -> p h d", d=128),
                                 [banks[i]], [Vst])
                    cx.dma("sp", Va_d[:, :, b * 4:(b + 1) * 4, :].rearrange("h p t d -> p h t d"), Vst[:], Vst, False)
                    heads = [("q", h) for h in range(8)] + [("k", i) for i in range(2)]
                    hstate = {}

                    def stage_a(n):
                        kind, h = heads[n]
                        i = next_mb()
                        win_mm(i, 128, (C_QB if kind == "q" else C_KB) + h * 128, hT)
                        s = sq[st["sq"] % 2]
                        st["sq"] += 1
                        cx.op("act", lambda e: e.activation(out=s[:], in_=bk(i), func=AF.Square), reads=[banks[i]], writes=[s])
                        hstate[n] = (i, s)

                    def stage_b(n):
                        kind, h = heads[n]
                        i, s = hstate[n]
                        j = next_ss()
                        cx.mm(banks[j], bk(j), [(ones[:], s[:])], [ones, s])
                        rq = rsq[n % 2]
                        cx.op("act", lambda e: e.activation(out=rq[:], in_=bk(j), func=AF.Sqrt, bias=epst[:, 0:1], scale=1.0 / 128),
                              reads=[banks[j], epst], writes=[rq])
                        cx.op("dve", lambda e: e.reciprocal(out=rq[:], in_=rq[:]), reads=[rq], writes=[rq])
                        u = uu[n % 2]
                        g = gqn if kind == "q" else gkn
                        cx.op("dve", lambda e: e.scalar_tensor_tensor(out=u[:], in0=bk(i), scalar=g[:, 0:1], in1=rq[:],
                                                                      op0=ALU.mult, op1=ALU.mult),
                              reads=[banks[i], g, rq], writes=[u])
                        r = next_ring()
                        rope(lambda a, b_: u[a:b_, :], u, False, 128, tB_, r[:], r)
                        dst = Qb_d if kind == "q" else Kb_d
                        cx.dma("sp", dst[h, :, t0:t0 + TB], r[:], r, False)

                    stage_a(0)
                    for n in range(len(heads)):
                        if n + 1 < len(heads):
                            stage_a(n + 1)
                        stage_b(n)
                    for t in range(4):
                        i = next_mb()
                        cx.mm(banks[i], bk(i, 128, 0, 256), [(hT[:, kc, t * 128:(t + 1) * 128], Win[:, kc, C_VB:C_VB + 256])
                                                              for kc in range(NCH)], [Win, hT])
                        evac("act" if t % 2 else "dve", Vbst[:, :, t, :], bk(i, 128, 0, 256).rearrange("p (h d) -> p h d", d=128),
                             [banks[i]], [Vbst])
                    cx.dma("sp", Vb_d[:, :, b * 4:(b + 1) * 4, :].rearrange("h p t d -> p h t d"), Vbst[:], Vbst, False)
            cx.barrier()

        def phase_A(l):
            with ExitStack() as pe_:
                NKMAX = 2 * SEG
                KaN = [sb(pe_, "a_kan%d" % i, [128, NKMAX], BF16) for i in range(2)]
                Va = [sb(pe_, "a_va%d" % i, [128, NKMAX // 128, 128], BF16) for i in range(2)]
                KRl = [sb(pe_, "a_kr%d" % i, [64, NKMAX], BF16) for i in range(2)]
                Kb = [sb(pe_, "a_kb%d" % i, [128, NKMAX], BF16) for i in range(2)]
                Vb = [sb(pe_, "a_vb%d" % i, [128, NKMAX // 128, 128], BF16) for i in range(2)]
                qn = [sb(pe_, "a_qn%d" % i, [128, TB], BF16) for i in range(3)]
                qr = [sb(pe_, "a_qr%d" % i, [64, TB], BF16) for i in range(3)]
                qb = [sb(pe_, "a_qb%d" % i, [128, TB], BF16) for i in range(3)]
                pT = [sb(pe_, "a_pt%d" % i, [128, 2 * TB], BF16) for i in range(3)]
                rec = [sb(pe_, "a_rec%d" % i, [128, TB], F32) for i in range(2)]
                ost = [sb(pe_, "a_ost%d" % i, [128, TB], BF16) for i in range(2)]
                slots = [(0, 2 * SEG), (2 * SEG, SEG)]
                jobs = []
                for s, (k0, nk) in enumerate(slots):
                    for h in range(8):
                        for qi in range(nk // TB):
                            jobs.append(dict(kind="a", s=s, h=h, qi=qi, t0=k0 + qi * TB, k0=k0, nk=nk))
                            jobs.append(dict(kind="b", s=s, h=h, qi=qi, t0=k0 + qi * TB, k0=k0, nk=nk))
                for ji, J in enumerate(jobs):
                    J["ji"] = ji
                    J["kidx"] = (J["s"] * 8 + J["h"]) % 2
                    J["gidx"] = (J["s"] * 2 + J["h"] // 4) % 2
                    J["q3"] = (ji // 2) % 3
                    J["np"] = J["nk"] // 256
                    J["loaded"] = False
                CH = 4096

                def ensure_loaded(ji):
                    if ji >= len(jobs) or jobs[ji]["loaded"]:
                        return
                    J = jobs[ji]
                    J["loaded"] = True
                    h, k0, nk, t0 = J["h"], J["k0"], J["nk"], J["t0"]
                    if J["kind"] == "a":
                        if J["qi"] == 0:
                            if h == 0:
                                for c0 in range(0, nk, CH):
                                    c1 = min(nk, c0 + CH)
                                    cx.dma("sp", KRl[J["s"] % 2][:, c0:c1], KR_d[:, k0 + c0:k0 + c1], KRl[J["s"] % 2], True)
                            kb_, vb_ = KaN[J["kidx"]], Va[J["kidx"]]
                            for c0 in range(0, nk, CH):
                                c1 = min(nk, c0 + CH)
                                cx.dma("sp", kb_[:, c0:c1], KaN_d[h, :, k0 + c0:k0 + c1], kb_, True)
                                cx.dma("sp", vb_[:, c0 // 128:c1 // 128, :], Va_d[h, :, (k0 + c0) // 128:(k0 + c1) // 128, :], vb_, True)
                            if h % 4 == 0:
                                kb_, vb_ = Kb[J["gidx"]], Vb[J["gidx"]]
                                for c0 in range(0, nk, CH):
                                    c1 = min(nk, c0 + CH)
                                    cx.dma("sp", kb_[:, c0:c1], Kb_d[h // 4, :, k0 + c0:k0 + c1], kb_, True)
                                    cx.dma("sp", vb_[:, c0 // 128:c1 // 128, :], Vb_d[h // 4, :, (k0 + c0) // 128:(k0 + c1) // 128, :], vb_, True)
                        cx.dma("sp", qn[J["q3"]][:], QaN_d[h, :, t0:t0 + TB], qn[J["q3"]], True)
                        cx.dma("sp", qr[J["q3"]][:], QaR_d[h, :, t0:t0 + TB], qr[J["q3"]], True)
                    else:
                        cx.dma("sp", qb[J["q3"]][:], Qb_d[h, :, t0:t0 + TB], qb[J["q3"]], True)

                pairs = [(ji, p) for ji, J in enumerate(jobs) for p in range(J["np"])]

                def emit_S(idx):
                    ji, p = pairs[idx]
                    if p == 0:
                        for d_ in range(3):
                            ensure_loaded(ji + d_)
                    J = jobs[ji]
                    par = idx % 2
                    for half in range(2):
                        kc = 2 * p + half
                        i = 2 * par + half
                        if J["kind"] == "a":
                            K_, Q_, R_ = KaN[J["kidx"]], qn[J["q3"]], qr[J["q3"]]
                            KRs = KRl[J["s"] % 2]
                            cx.mm(banks[i], bk(i), [(K_[:, kc * 128:(kc + 1) * 128], Q_[:]),
                                                    (KRs[0:64, kc * 128:(kc + 1) * 128], R_[0:64, :])], [K_, Q_, KRs, R_])
                        else:
                            K_, Q_ = Kb[J["gidx"]], qb[J["q3"]]
                            cx.mm(banks[i], bk(i), [(K_[:, kc * 128:(kc + 1) * 128], Q_[:])], [K_, Q_])

                def emit_rest(idx):
                    ji, p = pairs[idx]
                    J = jobs[ji]
                    par = idx % 2
                    P_ = pT[idx % 3]
                    qseg = J["t0"] // SEG
                    kseg = (J["k0"] + p * 256) // SEG
                    mc = 0 if qseg == kseg else 1
                    scale = (192.0 if J["kind"] == "a" else 128.0) ** -0.5
                    cx.op("act", lambda e: e.activation(out=P_[:], in_=ps_t[:, 2 * par * 512:2 * par * 512 + 1024], func=AF.Exp,
                                                        bias=maskb[:, mc:mc + 1], scale=scale),
                          reads=[banks[2 * par], banks[2 * par + 1], maskb], writes=[P_])
                    jp = ji % 2
                    io, isum = 4 + jp, 6 + jp
                    V_ = Va[J["kidx"]] if J["kind"] == "a" else Vb[J["gidx"]]
                    NP = J["np"]
                    for half in range(2):
                        kc = 2 * p + half
                        first = (p == 0 and half == 0)
                        last = (p == NP - 1 and half == 1)
                        cx.mm(banks[io], bk(io), [(V_[:, kc, :], P_[:, half * TB:(half + 1) * TB])], [V_, P_], start=first, stop=last)
                        cx.mm(banks[isum], bk(isum), [(ones[:], P_[:, half * TB:(half + 1) * TB])], [ones, P_], start=first, stop=last)
                    if p == NP - 1:
                        rc, o_ = rec[jp], ost[jp]
                        cx.op("dve", lambda e: e.reciprocal(out=rc[:], in_=bk(isum)), reads=[banks[isum]], writes=[rc])
                        cx.op("dve", lambda e: e.tensor_tensor(out=o_[:], in0=bk(io), in1=rc[:], op=ALU.mult),
                              reads=[banks[io], rc], writes=[o_])
                        dst = Oa_d if J["kind"] == "a" else Ob_d
                        cx.dma("sp", dst[J["h"], :, J["t0"]:J["t0"] + TB], o_[:], o_, False)

                emit_S(0)
                if len(pairs) > 1:
                    emit_S(1)
                for idx in range(len(pairs)):
                    emit_rest(idx)
                    if idx + 2 < len(pairs):
                        emit_S(idx + 2)
            cx.barrier()

        def phase_O(l):
            with ExitStack() as pe_:
                Wo = sb(pe_, "o_w", [128, NCH, D], BF16)
                xb = [sb(pe_, "o_x%d" % i, [128, NCH, TB], F32) for i in range(2)]
                oa = [sb(pe_, "o_oa%d" % i, [128, NCH, TB], BF16) for i in range(2)]
                ob = [sb(pe_, "o_ob%d" % i, [128, NCH, TB], BF16) for i in range(2)]
                ga = [sb(pe_, "o_ga%d" % i, [128, NCH, TB], BF16) for i in range(2)]
                gb = [sb(pe_, "o_gb%d" % i, [128, NCH, TB], BF16) for i in range(2)]
                sa = sb(pe_, "o_sa", [128, NCH, TB], F32)
                sb_ = sb(pe_, "o_sb", [128, NCH, TB], F32)
                mg = [sb(pe_, "o_mg%d" % i, [128, NCH, TB], BF16) for i in range(2)]
                for kc in range(NCH):
                    cx.dma("pool", Wo[:, kc, :], Wd["w_o"][l][kc * 128:(kc + 1) * 128, :], Wo, True)

                def blk(dt, b):
                    return dt[:, :, b * TB:(b + 1) * TB].rearrange("c p t -> p c t")

                def load(b):
                    k = b % 2
                    cx.dma("sp", ga[k][:], blk(Ga_d, b), ga[k], True)
                    cx.dma("sp", gb[k][:], blk(Gb_d, b), gb[k], True)
                    cx.dma("sp", oa[k][:], blk(Oa_d, b), oa[k], True)
                    cx.dma("sp", ob[k][:], blk(Ob_d, b), ob[k], True)
                    cx.dma("sp", xb[k][:], xT_blk(b), xb[k], True)

                load(0)
                for b in range(NB):
                    k = b % 2
                    if b + 1 < NB:
                        load(b + 1)
                    x, m = xb[k], mg[k]
                    cx.op("act", lambda e: e.activation(out=sa[:], in_=ga[k][:], func=AF.Sigmoid), reads=[ga[k]], writes=[sa])
                    cx.op("act", lambda e: e.activation(out=sb_[:], in_=gb[k][:], func=AF.Sigmoid), reads=[gb[k]], writes=[sb_])
                    cx.op("dve", lambda e: e.tensor_tensor(out=sa[:], in0=sa[:], in1=oa[k][:], op=ALU.mult), reads=[sa, oa[k]], writes=[sa])
                    cx.op("pool", lambda e: e.tensor_tensor(out=sb_[:], in0=sb_[:], in1=ob[k][:], op=ALU.mult), reads=[sb_, ob[k]], writes=[sb_])
                    cx.op("dve", lambda e: e.tensor_tensor(out=m[:], in0=sa[:], in1=sb_[:], op=ALU.add), reads=[sa, sb_], writes=[m])
                    for c in range(NCH):
                        i = c % 4
                        cx.mm(banks[i], bk(i), [(Wo[:, kc, c * 128:(c + 1) * 128], m[:, kc, :]) for kc in range(NCH)], [Wo, m])
                        cx.op("dve", lambda e, c=c, i=i: e.tensor_tensor(out=x[:, c, :], in0=bk(i), in1=x[:, c, :], op=ALU.add),
                              reads=[banks[i], x], writes=[x])
                    cx.dma("sp", xT_blk(b), x[:], x, False)
            cx.barrier()

        phase_X0()
        for l in range(DEPTH):
            if phases is None or "F1" in phases:
                phase_F(l, 1)
            if phases is None or "M" in phases or "P" in phases:
                phase_P(l)
            if phases is None or "M" in phases or "A" in phases:
                phase_A(l)
            if phases is None or "M" in phases or "O" in phases:
                phase_O(l)
            if phases is None or "F2" in phases:
                phase_F(l, 2)
        phase_XF()
    return nc


def _rope_tables(pos, rot_dim):
    n = rot_dim // 4
    row = (pos // 64).astype(np.float32)
    col = (pos % 64).astype(np.float32)
    freqs = (np.float32(10000.0) ** (-np.arange(n, dtype=np.float32) / np.float32(n))).astype(np.float32)
    ang = np.concatenate([row[:, None] * freqs, col[:, None] * freqs], axis=-1).astype(np.float32)
    cos = np.cos(ang).astype(np.float32).T
    sin = np.sin(ang).astype(np.float32).T
    return np.ascontiguousarray(np.stack([np.concatenate([cos, cos], 0), np.concatenate([-sin, sin], 0)], 0))


def _gain_layout(g, nchunk):
    L = g.shape[0]
    return np.ascontiguousarray(g.reshape(L, nchunk, 128).transpose(0, 2, 1))


_PROG_CACHE = {}


def kernel_impl(inputs, SEG, DEPTH, phases=None, trace=False):
    xp = np.asarray(inputs["x_prompt"], dtype=np.float32)
    xs = np.asarray(inputs["x_sample"], dtype=np.float32)
    assert xp.shape[1] == 2 * SEG and xs.shape[1] == SEG
    T = 3 * SEG
    key = (SEG, DEPTH, None if phases is None else tuple(phases))
    if key not in _PROG_CACHE:
        _PROG_CACHE[key] = build_program(SEG, DEPTH, phases)
    nc = _PROG_CACHE[key]
    f = lambda a: np.ascontiguousarray(np.asarray(a, dtype=np.float32))
    shared = {
        "norm_ffn1": _gain_layout(f(inputs["norm_ffn1"])[:max(DEPTH, 1)], 8),
        "w_ffn1_in": f(inputs["w_ffn1_in"])[:max(DEPTH, 1)],
        "w_ffn1_out": f(inputs["w_ffn1_out"])[:max(DEPTH, 1)],
        "norm_mix": _gain_layout(f(inputs["norm_mix"])[:max(DEPTH, 1)], 8),
        "w_in": f(inputs["w_in"])[:max(DEPTH, 1)],
        "g_cq": _gain_layout(f(inputs["g_cq"])[:max(DEPTH, 1)], 3),
        "w_uq": f(inputs["w_uq"])[:max(DEPTH, 1)],
        "g_ckv": _gain_layout(f(inputs["g_ckv"])[:max(DEPTH, 1)], 2),
        "w_ukv": f(inputs["w_ukv"])[:max(DEPTH, 1)],
        "g_qn": _gain_layout(f(inputs["g_qn"])[:max(DEPTH, 1)], 1),
        "g_kn": _gain_layout(f(inputs["g_kn"])[:max(DEPTH, 1)], 1),
        "w_o": f(inputs["w_o"])[:max(DEPTH, 1)],
        "norm_ffn2": _gain_layout(f(inputs["norm_ffn2"])[:max(DEPTH, 1)], 8),
        "w_ffn2_in": f(inputs["w_ffn2_in"])[:max(DEPTH, 1)],
        "w_ffn2_out": f(inputs["w_ffn2_out"])[:max(DEPTH, 1)],
        "norm_final": np.ascontiguousarray(np.broadcast_to(f(inputs["norm_final"]).reshape(1, D), (128, D))),
        "ident": np.eye(128, dtype=np.float32),
    }
    pos_seq = np.arange(SEG, dtype=np.int64)
    in_maps = []
    for c in range(N_CORES):
        if c < 4:
            long_x = xp[c]
            pos = np.concatenate([pos_seq, pos_seq + SEG, pos_seq])
            cross = 0.0
        else:
            long_x = np.concatenate([xs[2 * (c - 4)], xs[2 * (c - 4) + 1]], axis=0)
            pos = np.concatenate([pos_seq, pos_seq, pos_seq])
            cross = -30000.0
        xin = np.ascontiguousarray(np.concatenate([long_x, xs[8 + c]], axis=0))
        mb = np.zeros((128, 2), np.float32)
        mb[:, 1] = cross
        m = dict(shared)
        m["xin"] = xin
        m["ropeA"] = _rope_tables(pos, 64)
        m["ropeB"] = _rope_tables(pos, 128)
        m["maskb"] = mb
        in_maps.append(m)
    res = run_bass_kernel_spmd(nc, in_maps, core_ids=list(range(N_CORES)), trace=trace)
    yp = np.empty_like(xp)
    ys = np.empty_like(xs)
    for c in range(N_CORES):
        y = res.results[c]["yout"]
        if c < 4:
            yp[c] = y[:2 * SEG]
        else:
            ys[2 * (c - 4)] = y[:SEG]
            ys[2 * (c - 4) + 1] = y[SEG:2 * SEG]
        ys[8 + c] = y[2 * SEG:]
    if trace:
        return (yp, ys), res
    return (yp, ys)


def kernel(**inputs):
    return kernel_impl(inputs, 4096, 4)
```
